# Optimizing a Trainium2 kernel written in Bass

```python
import math
import jax
import jax.numpy as jnp
from jax import lax
import numpy as np

D_MODEL = 2048
BATCH = 4
SEQ = 2048
DEPTH = 2

PLE_DIM = 256
CHUNK = 64
N_EVEN = (DEPTH + 1) // 2
N_ODD = DEPTH // 2
MIX_WIDTH = D_MODEL
GROUP_WIDTH = MIX_WIDTH // 2

HGRN_HEAD_DIM = 128
HGRN_HEADS = GROUP_WIDTH // HGRN_HEAD_DIM

GLA_HEADS = 4
GLA_DV = GROUP_WIDTH // GLA_HEADS
GLA_DK = GLA_DV // 2
GLA_K_WIDTH = GLA_HEADS * GLA_DK
GLA_GATE_RANK = 16
GLA_GATE_TAU = 16.0

RWKV_HEAD_DIM = 64
RWKV_HEADS = GROUP_WIDTH // RWKV_HEAD_DIM
RWKV_DECAY_LORA = max(32, int(round(1.8 * D_MODEL ** 0.5 / 32)) * 32)
RWKV_AAA_LORA = max(32, int(round(1.8 * D_MODEL ** 0.5 / 32)) * 32)
RWKV_GATE_LORA = max(32, int(round(0.6 * D_MODEL ** 0.8 / 32)) * 32)
RWKV_GN_EPS = 64e-5

GDN_HEAD_DIM = 128
GDN_HEADS = GROUP_WIDTH // GDN_HEAD_DIM
GDN_CONV = 4

FFN_DIM = 5632
FFN_CONV = 3

ALPHA = (2 * DEPTH) ** 0.25
BETA = (8 * DEPTH) ** -0.25
LN_EPS = 1e-5
RMS_EPS = 1e-6
L2_EPS = 1e-6

L0_SPLITS = (GROUP_WIDTH, GROUP_WIDTH, GROUP_WIDTH, GROUP_WIDTH,
             GLA_K_WIDTH, GLA_K_WIDTH, GROUP_WIDTH, GLA_GATE_RANK, GROUP_WIDTH)
L0_COLS = sum(L0_SPLITS)
RWKV_SPLITS = (GROUP_WIDTH, GROUP_WIDTH, GROUP_WIDTH, RWKV_DECAY_LORA, RWKV_AAA_LORA, RWKV_GATE_LORA)
RWKV_COLS = sum(RWKV_SPLITS)
GDN_SPLITS = (3 * GROUP_WIDTH, GROUP_WIDTH, GDN_HEADS, GDN_HEADS)
GDN_COLS = sum(GDN_SPLITS)
L1_COLS = RWKV_COLS + GDN_COLS

kernel_name = "hybrid_hgrn2_gla_rwkv7_gdn_deepnorm"


def split_cols(t, sizes):
    idx = [int(s) for s in np.cumsum(sizes)[:-1]]
    return jnp.split(t, idx, axis=-1)


def to_heads(t, n_heads):
    b, s, _ = t.shape
    return t.reshape(b, s, n_heads, -1).transpose(0, 2, 1, 3)


def from_heads(t):
    b, h, s, d = t.shape
    return t.transpose(0, 2, 1, 3).reshape(b, s, h * d)


def layer_norm(x, w, b):
    xf = x.astype(jnp.float32)
    mu = jnp.mean(xf, -1, keepdims=True)
    var = jnp.mean(jnp.square(xf - mu), -1, keepdims=True)
    return ((xf - mu) * lax.rsqrt(var + LN_EPS) * w + b).astype(x.dtype)


def head_rmsnorm(o, w):
    o = o * lax.rsqrt(jnp.mean(jnp.square(o), -1, keepdims=True) + RMS_EPS)
    return from_heads(o) * w


def l2norm(t):
    return t * lax.rsqrt(jnp.sum(jnp.square(t), -1, keepdims=True) + L2_EPS)


def causal_dwconv(x, w):
    width = w.shape[0]
    return lax.conv_general_dilated(
        x, w.astype(x.dtype)[:, None, :], window_strides=(1,), padding=[(width - 1, 0)],
        dimension_numbers=("NWC", "WIO", "NWC"), feature_group_count=x.shape[-1])


def chunk_gla(q, k, v, g):
    B, H, T, K = q.shape
    V = v.shape[-1]
    N = T // CHUNK
    q, k, g = (t.reshape(B, H, N, CHUNK, K) for t in (q, k, g))
    v = v.reshape(B, H, N, CHUNK, V)
    G = jnp.cumsum(g, axis=3)
    G_ref = G[:, :, :, CHUNK // 2:CHUNK // 2 + 1]
    G_last = G[:, :, :, -1:]
    causal = jnp.tril(jnp.ones((CHUNK, CHUNK), dtype=bool))
    a_qk = jnp.einsum("bhnik,bhnjk->bhnij", q * jnp.exp(G - G_ref), k * jnp.exp(G_ref - G))
    o_intra = jnp.einsum("bhnij,bhnjv->bhniv", jnp.where(causal, a_qk, 0.0), v)
    u = jnp.einsum("bhnck,bhncv->bhnkv", k * jnp.exp(G_last - G), v)
    d = jnp.exp(G_last[:, :, :, 0])

    def step(S, inp):
        d_n, u_n = inp
        return S * d_n[..., None] + u_n, S

    _, S_prev = lax.scan(step, jnp.zeros((B, H, K, V), q.dtype),
                         (jnp.moveaxis(d, 2, 0), jnp.moveaxis(u, 2, 0)))
    o_inter = jnp.einsum("bhnik,bhnkv->bhniv", q * jnp.exp(G), jnp.moveaxis(S_prev, 0, 2))
    return (o_intra + o_inter).reshape(B, H, T, V)


def chunk_gated_delta(q, k, v, g, beta):
    B, H, T, K = q.shape
    V = v.shape[-1]
    N = T // CHUNK
    q, k = (t.reshape(B, H, N, CHUNK, K) for t in (q, k))
    v = v.reshape(B, H, N, CHUNK, V)
    g, beta = (t.reshape(B, H, N, CHUNK) for t in (g, beta))
    G = jnp.cumsum(g, axis=-1)
    incl = jnp.tril(jnp.ones((CHUNK, CHUNK), dtype=bool))
    strict = jnp.tril(jnp.ones((CHUNK, CHUNK), dtype=bool), k=-1)
    diff = G[..., :, None] - G[..., None, :]
    gam = jnp.where(incl, jnp.exp(jnp.where(incl, diff, 0.0)), 0.0)
    kb = k * beta[..., None]
    L = jnp.where(strict, jnp.einsum("bhnik,bhnjk->bhnij", kb, k) * gam, 0.0)
    eye = jnp.eye(CHUNK, dtype=q.dtype)
    Tinv = lax.linalg.triangular_solve(L + eye, jnp.broadcast_to(eye, L.shape), left_side=True,
                                       lower=True, unit_diagonal=True)
    u = jnp.einsum("bhnij,bhnjv->bhniv", Tinv, v * beta[..., None])
    w = jnp.einsum("bhnij,bhnjk->bhnik", Tinv, kb * jnp.exp(G)[..., None])
    a_qk = jnp.where(incl, jnp.einsum("bhnik,bhnjk->bhnij", q, k) * gam, 0.0)
    qg = q * jnp.exp(G)[..., None]
    kg = k * jnp.exp(G[..., -1:] - G)[..., None]
    d_last = jnp.exp(G[..., -1])

    def step(S, inp):
        u_n, w_n, aqk_n, qg_n, kg_n, d_n = inp
        v_new = u_n - jnp.einsum("bhck,bhkv->bhcv", w_n, S)
        o_n = jnp.einsum("bhck,bhkv->bhcv", qg_n, S) + jnp.einsum("bhij,bhjv->bhiv", aqk_n, v_new)
        S = S * d_n[..., None, None] + jnp.einsum("bhck,bhcv->bhkv", kg_n, v_new)
        return S, o_n

    xs = tuple(jnp.moveaxis(t, 2, 0) for t in (u, w, a_qk, qg, kg, d_last))
    _, o = lax.scan(step, jnp.zeros((B, H, K, V), q.dtype), xs)
    return jnp.moveaxis(o, 0, 2).reshape(B, H, T, V)


def rwkv7_scan(r, w, k, v, a, b):
    B, T, H, N = r.shape

    def step(S, inp):
        r_t, w_t, k_t, v_t, a_t, b_t = inp
        sa = jnp.einsum("bhvk,bhk->bhv", S, a_t)
        S = S * w_t[:, :, None, :] + sa[..., None] * b_t[:, :, None, :] + v_t[..., None] * k_t[:, :, None, :]
        return S, jnp.einsum("bhvk,bhk->bhv", S, r_t)

    xs = tuple(jnp.moveaxis(t, 1, 0) for t in (r, w, k, v, a, b))
    _, y = lax.scan(step, jnp.zeros((B, H, N, N), r.dtype), xs)
    return jnp.moveaxis(y, 0, 1)


def mixer_hgrn2_gla(x, lb, w_in, gla_w2, gla_b, hgrn_norm, gla_norm, w_out):
    proj = (x @ w_in).astype(jnp.float32)
    hq, hf, hi, hg, gq, gk, gv, gd, gr = split_cols(proj, L0_SPLITS)
    f = lb + (1.0 - lb) * jax.nn.sigmoid(hf)
    o_h = chunk_gla(to_heads(jax.nn.silu(hq), HGRN_HEADS), to_heads(1.0 - f, HGRN_HEADS),
                    to_heads(hi, HGRN_HEADS), to_heads(jnp.log(f), HGRN_HEADS))
    o_h = head_rmsnorm(o_h, hgrn_norm) * jax.nn.silu(hg)
    g_log = jax.nn.log_sigmoid(gd @ gla_w2 + gla_b) / GLA_GATE_TAU
    o_g = chunk_gla(to_heads(gq * GLA_DK ** -0.5, GLA_HEADS), to_heads(gk, GLA_HEADS),
                    to_heads(gv, GLA_HEADS), to_heads(g_log, GLA_HEADS))
    o_g = head_rmsnorm(o_g, gla_norm) * jax.nn.silu(gr)
    return jnp.concatenate([o_h, o_g], axis=-1).astype(x.dtype) @ w_out


def mixer_rwkv7_gdn(x, w_in, mu, w0, w2, a0, a2, g2, k_k, k_a, r_k, gn_w, gn_b,
                    conv_w, a_log, dt_bias, gdn_norm, w_out):
    B, T, _ = x.shape
    proj = (x @ w_in).astype(jnp.float32)
    pr, pd = proj[..., :RWKV_COLS], proj[..., RWKV_COLS:]
    prev = jnp.pad(pr, ((0, 0), (1, 0), (0, 0)))[:, :-1]
    pr = pr + (prev - pr) * mu
    r, k, v, wd, ad, gd = split_cols(pr, RWKV_SPLITS)
    decay = jnp.exp(-jnp.exp(-jax.nn.softplus(-(w0 + jnp.tanh(wd) @ w2)) - 0.5))
    a = jax.nn.sigmoid(a0 + ad @ a2)
    gate = jax.nn.sigmoid(gd) @ g2
    hs = lambda t: t.reshape(B, T, RWKV_HEADS, RWKV_HEAD_DIM)
    kk = l2norm(hs(k * k_k))
    k = k * (1.0 + (a - 1.0) * k_a)
    rh, kh, vh = hs(r), hs(k), hs(v)
    y = rwkv7_scan(rh, hs(decay), kh, vh, -kk, kk * hs(a))
    mean = jnp.mean(y, -1, keepdims=True)
    var = jnp.mean(jnp.square(y - mean), -1, keepdims=True)
    y = ((y - mean) * lax.rsqrt(var + RWKV_GN_EPS)).reshape(B, T, GROUP_WIDTH) * gn_w + gn_b
    bonus = (jnp.sum(rh * kh * r_k, -1, keepdims=True) * vh).reshape(B, T, GROUP_WIDTH)
    o_c = (y + bonus) * gate
    qkv_raw, dz, da, db = split_cols(pd, GDN_SPLITS)
    qkv = jax.nn.silu(causal_dwconv(qkv_raw, conv_w.astype(jnp.float32)))
    dq, dk, dv = jnp.split(qkv, 3, axis=-1)
    qh = l2norm(to_heads(dq, GDN_HEADS)) * GDN_HEAD_DIM ** -0.5
    kh2 = l2norm(to_heads(dk, GDN_HEADS))
    vh2 = to_heads(dv, GDN_HEADS)
    beta = jax.nn.sigmoid(db).transpose(0, 2, 1)
    g = (-jnp.exp(a_log) * jax.nn.softplus(da + dt_bias)).transpose(0, 2, 1)
    o_d = chunk_gated_delta(qh, kh2, vh2, g, beta)
    o_d = head_rmsnorm(o_d, gdn_norm) * jax.nn.silu(dz)
    return jnp.concatenate([o_c, o_d], axis=-1).astype(x.dtype) @ w_out


def conv_ffn(x, w_up, conv_w, w_down):
    u = causal_dwconv(x @ w_up, conv_w)
    gate, val = jnp.split(u, 2, axis=-1)
    return (jax.nn.silu(gate) * val) @ w_down


def setup_inputs(seed: int = 0) -> dict:
    key = jax.random.key(seed)
    ks = iter(jax.random.split(key, 48))
    f32 = jnp.float32
    nrm = lambda shape, s: jax.random.normal(next(ks), shape, f32) * s
    gain = lambda shape: 1.0 + 0.02 * jax.random.normal(next(ks), shape, f32)
    D, W = D_MODEL, GROUP_WIDTH
    dt = jnp.exp(jax.random.uniform(next(ks), (N_ODD, GDN_HEADS), f32, math.log(1e-3), math.log(1e-1)))
    return {
        "x": nrm((BATCH, SEQ, D), 1.0),
        "p": nrm((DEPTH, BATCH, SEQ, PLE_DIM), 1.0),
        "hgrn_lb_logits": nrm((DEPTH + 1, W), 0.1),
        "e_w_in": nrm((N_EVEN, D, L0_COLS), D ** -0.5),
        "e_gla_w2": nrm((N_EVEN, GLA_GATE_RANK, GLA_K_WIDTH), GLA_GATE_RANK ** -0.5),
        "e_gla_b": nrm((N_EVEN, GLA_K_WIDTH), 0.1),
        "e_hgrn_norm": gain((N_EVEN, W)),
        "e_gla_norm": gain((N_EVEN, W)),
        "e_w_out": nrm((N_EVEN, MIX_WIDTH, D), MIX_WIDTH ** -0.5 * BETA),
        "o_w_in": nrm((N_ODD, D, L1_COLS), D ** -0.5),
        "o_rwkv_mu": jax.random.uniform(next(ks), (N_ODD, RWKV_COLS), f32),
        "o_rwkv_w0": jax.random.uniform(next(ks), (N_ODD, W), f32, -6.0, 1.0),
        "o_rwkv_w2": nrm((N_ODD, RWKV_DECAY_LORA, W), RWKV_DECAY_LORA ** -0.5),
        "o_rwkv_a0": nrm((N_ODD, W), 0.1),
        "o_rwkv_a2": nrm((N_ODD, RWKV_AAA_LORA, W), RWKV_AAA_LORA ** -0.5),
        "o_rwkv_g2": nrm((N_ODD, RWKV_GATE_LORA, W), RWKV_GATE_LORA ** -0.5),
        "o_rwkv_kk": 0.85 + nrm((N_ODD, W), 0.02),
        "o_rwkv_ka": gain((N_ODD, W)),
        "o_rwkv_rk": nrm((N_ODD, RWKV_HEADS, RWKV_HEAD_DIM), 0.1),
        "o_rwkv_gn_w": gain((N_ODD, W)),
        "o_rwkv_gn_b": nrm((N_ODD, W), 0.02),
        "o_gdn_conv": nrm((N_ODD, GDN_CONV, 3 * W), GDN_CONV ** -0.5),
        "o_gdn_a_log": jnp.log(jax.random.uniform(next(ks), (N_ODD, GDN_HEADS), f32, 1.0, 16.0)),
        "o_gdn_dt_bias": dt + jnp.log(-jnp.expm1(-dt)),
        "o_gdn_norm": gain((N_ODD, W)),
        "o_w_out": nrm((N_ODD, MIX_WIDTH, D), MIX_WIDTH ** -0.5 * BETA),
        "ln_mix_w": gain((DEPTH, D)),
        "ln_mix_b": nrm((DEPTH, D), 0.02),
        "ln_ffn_w": gain((DEPTH, D)),
        "ln_ffn_b": nrm((DEPTH, D), 0.02),
        "ffn_w_up": nrm((DEPTH, D, 2 * FFN_DIM), D ** -0.5),
        "ffn_conv": nrm((DEPTH, FFN_CONV, 2 * FFN_DIM), FFN_CONV ** -0.5),
        "ffn_w_down": nrm((DEPTH, FFN_DIM, D), FFN_DIM ** -0.5 * BETA),
        "ple_w_proj": nrm((DEPTH, PLE_DIM, D), PLE_DIM ** -0.5),
        "ple_w_gate": nrm((DEPTH, D, D), D ** -0.5),
    }


def reference(x, p, hgrn_lb_logits, e_w_in, e_gla_w2, e_gla_b, e_hgrn_norm, e_gla_norm, e_w_out,
              o_w_in, o_rwkv_mu, o_rwkv_w0, o_rwkv_w2, o_rwkv_a0, o_rwkv_a2, o_rwkv_g2, o_rwkv_kk,
              o_rwkv_ka, o_rwkv_rk, o_rwkv_gn_w, o_rwkv_gn_b, o_gdn_conv, o_gdn_a_log, o_gdn_dt_bias,
              o_gdn_norm, o_w_out, ln_mix_w, ln_mix_b, ln_ffn_w, ln_ffn_b, ffn_w_up, ffn_conv,
              ffn_w_down, ple_w_proj, ple_w_gate):
    lb_table = jnp.cumsum(jax.nn.softmax(hgrn_lb_logits.astype(jnp.float32), axis=0), axis=0)
    for layer in range(DEPTH):
        j = layer // 2
        if layer % 2 == 0:
            mix = mixer_hgrn2_gla(x, lb_table[layer], e_w_in[j], e_gla_w2[j], e_gla_b[j],
                                  e_hgrn_norm[j], e_gla_norm[j], e_w_out[j])
        else:
            mix = mixer_rwkv7_gdn(x, o_w_in[j], o_rwkv_mu[j], o_rwkv_w0[j], o_rwkv_w2[j], o_rwkv_a0[j],
                                  o_rwkv_a2[j], o_rwkv_g2[j], o_rwkv_kk[j], o_rwkv_ka[j], o_rwkv_rk[j],
                                  o_rwkv_gn_w[j], o_rwkv_gn_b[j], o_gdn_conv[j], o_gdn_a_log[j],
                                  o_gdn_dt_bias[j], o_gdn_norm[j], o_w_out[j])
        x = layer_norm(ALPHA * x + mix, ln_mix_w[layer], ln_mix_b[layer])
        x = layer_norm(ALPHA * x + conv_ffn(x, ffn_w_up[layer], ffn_conv[layer], ffn_w_down[layer]),
                       ln_ffn_w[layer], ln_ffn_b[layer])
        x = x + jax.nn.sigmoid(x @ ple_w_gate[layer]) * (p[layer] @ ple_w_proj[layer])
    return x
```

```python
from contextlib import ExitStack
import numpy as np
import ml_dtypes
import concourse.bass as bass
import concourse.mybir as mybir
from concourse.bass_utils import run_bass_kernel_spmd


F32 = mybir.dt.float32
BF16 = mybir.dt.bfloat16
AF = mybir.ActivationFunctionType
ALU = mybir.AluOpType
AX = mybir.AxisListType

ENGS = ("pe", "act", "dve", "pool", "sp")


class View:
    __slots__ = ("buf", "ap")

    def __init__(self, buf, ap):
        self.buf = buf
        self.ap = ap


class Buf:
    def __init__(self, K, handle, name, space):
        self.K = K
        self.h = handle
        self.name = name
        self.space = space
        self.last_w = None
        self.readers = {}
        self.dsem = None
        self.wtag = None
        if K is not None:
            K.bufs.append(self)

    def __getitem__(self, idx):
        return View(self, self.h[idx])

    def v(self, ap):
        return View(self, ap)


class KB:
    def __init__(self, nc, stack, same_engine_sync=True):
        self.nc = nc
        self.stack = stack
        self.semstack = stack
        self.rec = {e: [] for e in ENGS}
        self.sems = {}
        self.semcnt = {}
        self.seen = {e: {} for e in ENGS}
        self.same_engine_sync = same_engine_sync
        self.epoch = {e: 0 for e in ENGS}
        self.LIM = 1500
        for e in ENGS:
            self._newsem("E_" + e + "#0")
        self.nbuf = 0
        self.ndma_sems = 0
        self.bufs = []
        self.nreset = 0
        self._newsem("RZ1")
        self._newsem("RZ2")
        self.psum_banks = []
        self.psum_i = 0

    def _newsem(self, key):
        s = self.semstack.enter_context(self.nc.semaphore(key))
        self.sems[key] = s
        self.semcnt[key] = 0
        return key

    def sb(self, shape, dtype=F32, name=None):
        self.nbuf += 1
        name = name or f"sb{self.nbuf}"
        h = self.stack.enter_context(self.nc.sbuf_tensor(name, list(shape), dtype))
        return Buf(self, h, name, "sb")

    def ps(self, shape, dtype=F32, name=None):
        self.nbuf += 1
        name = name or f"ps{self.nbuf}"
        h = self.stack.enter_context(self.nc.psum_tensor(name, list(shape), dtype))
        return Buf(self, h, name, "ps")

    def dram(self, name, shape, dtype=F32, kind="Internal"):
        h = self.nc.dram_tensor(name, list(shape), dtype, kind=kind)
        return Buf(self, h.ap() if hasattr(h, "ap") else h, name, "dram")

    def init_psum(self, n=8):
        self.psum_banks = [self.ps([128, 512], F32, name=f"bank{i}") for i in range(n)]

    def bank(self):
        b = self.psum_banks[self.psum_i % len(self.psum_banks)]
        self.psum_i += 1
        return b

    def _waits(self, eng, reads, writes, dma=False, dsem=None):
        need = {}

        def add(dep):
            if dep is None:
                return
            k, v = dep
            if need.get(k, 0) < v:
                need[k] = v

        for b in reads:
            add(b.last_w)
        for b in writes:
            lw = b.last_w
            if not (dma and lw is not None and lw[0] == dsem):
                add(lw)
            for r in b.readers.values():
                add(r)
        own = "E_" + eng + "#"
        for k, v in need.items():
            if k.startswith(own) and not dma:
                if eng == "pe" or not self.same_engine_sync:
                    continue
            if self.seen[eng].get(k, 0) >= v:
                continue
            self.seen[eng][k] = v
            sem = self.sems[k]
            self.rec[eng].append(("w", sem, v))

    def op(self, eng, fn, reads=(), writes=()):
        reads = [x.buf if isinstance(x, View) else x for x in reads]
        writes = [x.buf if isinstance(x, View) else x for x in writes]
        self._waits(eng, reads, writes)
        key = "E_" + eng + "#" + str(self.epoch[eng])
        if self.semcnt[key] >= self.LIM:
            self.epoch[eng] += 1
            key = self._newsem("E_" + eng + "#" + str(self.epoch[eng]))
        self.semcnt[key] += 1
        idx = self.semcnt[key]
        self.rec[eng].append(("i", fn, self.sems[key], 1))
        for b in reads:
            b.readers[key] = (key, idx)
        for b in writes:
            b.last_w = (key, idx)
            b.readers = {}

    def dma(self, q, out, in_, **kw):
        ob, ib = out.buf, in_.buf
        if ob.space == "dram" and ib.space != "dram":
            owner = ib
        else:
            owner = ob
        if owner.dsem is None or self.semcnt[owner.dsem] >= self.LIM:
            self.ndma_sems += 1
            owner.dsem = self._newsem(f"D{self.ndma_sems}_{owner.name}")
        k = owner.dsem
        self._waits(q, [ib], [ob], dma=True, dsem=k)
        self.semcnt[k] += 16
        val = self.semcnt[k]
        oap, iap = out.ap, in_.ap
        self.rec[q].append(("i", lambda e: e.dma_start(out=oap, in_=iap, **kw), self.sems[k], 16))
        ib.readers[k] = (k, val)
        ob.last_w = (k, val)
        ob.readers = {}

    def finish(self, bufs, eng="sp"):
        for b in bufs:
            deps = [b.last_w] + list(b.readers.values())
            for d in deps:
                if d is None:
                    continue
                k, v = d
                if self.seen[eng].get(k, 0) >= v:
                    continue
                self.seen[eng][k] = v
                self.rec[eng].append(("w", self.sems[k], v))

    def reset(self):
        self.barrier(full=True)
        self.nreset += 1
        n = self.nreset
        rz1, rz2 = self.sems["RZ1"], self.sems["RZ2"]
        for eng in ENGS:
            self.rec[eng].append(("x", lambda e: e.sem_inc(rz1, 1)))
        self.rec["sp"].append(("w", rz1, len(ENGS) * n))
        for k, sem in self.sems.items():
            if k in ("RZ1", "RZ2"):
                continue
            if self.semcnt[k] == 0:
                continue
            self.rec["sp"].append(("x", (lambda sm: (lambda e: e.sem_clear(sm)))(sem)))
            self.semcnt[k] = 0
        self.rec["sp"].append(("x", lambda e: e.sem_inc(rz2, 1)))
        for eng in ENGS:
            if eng != "sp":
                self.rec[eng].append(("w", rz2, n))
        self.seen = {e: {} for e in ENGS}
        for b in self.bufs:
            b.last_w = None
            b.readers = {}

    def barrier(self, full=False):
        for eng in ENGS:
            for k, v in self.semcnt.items():
                if v == 0 or (k.startswith("E_" + eng + "#") and not full) or k in ("RZ1", "RZ2"):
                    continue
                if self.seen[eng].get(k, 0) >= v:
                    continue
                self.seen[eng][k] = v
                self.rec[eng].append(("w", self.sems[k], v))

    def emit(self):
        self.barrier()
        nc = self.nc
        rec = self.rec

        def play(e, lst):
            for it in lst:
                if it[0] == "w":
                    e.wait_ge(it[1], it[2])
                elif it[0] == "x":
                    it[1](e)
                else:
                    it[1](e).then_inc(it[2], it[3])

        with nc.Block() as block:
            @block.tensor
            def _(e):
                play(e, rec["pe"])

            @block.scalar
            def _(e):
                play(e, rec["act"])

            @block.vector
            def _(e):
                play(e, rec["dve"])

            @block.gpsimd
            def _(e):
                play(e, rec["pool"])

            @block.sync
            def _(e):
                play(e, rec["sp"])
        self.rec = {e: [] for e in ENGS}

    def mm(self, out, lhsT, rhs, start=True, stop=True):
        o, l, r = out.ap, lhsT.ap, rhs.ap
        self.op("pe", lambda e: e.matmul(o, l, r, start=start, stop=stop),
                reads=[lhsT, rhs] + ([] if start else [out]), writes=[out])

    def tr(self, out, in_, ident):
        o, i, d = out.ap, in_.ap, ident.ap
        self.op("pe", lambda e: e.transpose(o, i, d), reads=[in_, ident], writes=[out])

    def act(self, out, in_, func, bias=None, scale=None, accum=None, eng="act"):
        o, i = out.ap, in_.ap
        kw = {}
        rd = [in_]
        wr = [out]
        if bias is not None:
            if isinstance(bias, View):
                kw["bias"] = bias.ap
                rd.append(bias)
            else:
                kw["bias"] = float(bias)
        if scale is not None:
            if isinstance(scale, View):
                kw["scale"] = scale.ap
                rd.append(scale)
            else:
                kw["scale"] = float(scale)
        if accum is not None:
            kw["accum_out"] = accum.ap
            wr.append(accum)
        self.op(eng, lambda e: e.activation(o, i, func, **kw), reads=rd, writes=wr)

    def ts(self, out, in0, s1, s2, op0, op1=None, accum=None, eng="dve"):
        o, i = out.ap, in0.ap
        rd = [in0]
        wr = [out]
        a1 = s1
        a2 = s2
        if isinstance(s1, View):
            a1 = s1.ap
            rd.append(s1)
        if isinstance(s2, View):
            a2 = s2.ap
            rd.append(s2)
        kw = {}
        if op1 is not None:
            kw["op1"] = op1
        if accum is not None:
            kw["accum_out"] = accum.ap
            wr.append(accum)
        self.op(eng, lambda e: e.tensor_scalar(o, i, a1, a2, op0, **kw), reads=rd, writes=wr)

    def tt(self, out, in0, in1, op, eng="dve"):
        o, a, b = out.ap, in0.ap, in1.ap
        self.op(eng, lambda e: e.tensor_tensor(o, a, b, op), reads=[in0, in1], writes=[out])

    def stt(self, out, in0, s, in1, op0, op1, accum=None, eng="dve"):
        o, a, b = out.ap, in0.ap, in1.ap
        rd = [in0, in1]
        wr = [out]
        sa = s
        if isinstance(s, View):
            sa = s.ap
            rd.append(s)
        kw = {}
        if accum is not None:
            kw["accum_out"] = accum.ap
            wr.append(accum)
        self.op(eng, lambda e: e.scalar_tensor_tensor(o, a, sa, b, op0, op1, **kw), reads=rd, writes=wr)

    def copy(self, out, in_, eng="dve"):
        o, i = out.ap, in_.ap
        if eng == "act":
            self.op(eng, lambda e: e.copy(o, i), reads=[in_], writes=[out])
        else:
            self.op(eng, lambda e: e.tensor_copy(o, i), reads=[in_], writes=[out])

    def memset(self, out, val, eng="dve"):
        o = out.ap
        self.op(eng, lambda e: e.memset(o, val), reads=[], writes=[out])

    def recip(self, out, in_):
        o, i = out.ap, in_.ap
        self.op("dve", lambda e: e.reciprocal(o, i), reads=[in_], writes=[out])

    def reduce(self, out, in_, op=None, axis=None, eng="dve"):
        o, i = out.ap, in_.ap
        op = op or ALU.add
        axis = axis or AX.X
        self.op(eng, lambda e: e.tensor_reduce(o, i, axis, op), reads=[in_], writes=[out])


C = 64
D = 2048
RST = 0
KT = D // 128

def make_consts():
    c = {}
    idx = np.arange(C)
    tri = (idx[None, :] <= idx[:, None]).astype(np.float32)
    ref = C // 2
    mref = tri - tri[ref:ref + 1, :]
    c["ident"] = np.eye(C, dtype=np.float32)
    c["mrefT"] = mref.T.copy()
    c["triT"] = tri.T.copy()
    c["mUi"] = tri.T.copy()
    c["mUs"] = (idx[:, None] < idx[None, :]).astype(np.float32)
    c["mLs"] = (idx[None, :] < idx[:, None]).astype(np.float32)
    rc = np.zeros((C, C), np.float32)
    rc[:, 0] = tri[ref]
    rc[:, 1] = 1.0 - tri[ref]
    c["refc"] = rc
    c["ones"] = np.ones((C, C), np.float32)
    names = list(c.keys())
    arr = np.concatenate([c[k] for k in names], axis=1)
    off = {k: i * C for i, k in enumerate(names)}
    return arr, off

CST, COFF = make_consts()


def load_xT(K, X, T, pad=0):
    xT = K.sb([128, KT, pad + T], BF16, name="xT")
    if pad:
        for kt in range(KT):
            K.memset(xT[:, kt, 0:pad], 0.0)
    _old = K.stack
    with ExitStack() as _tmp:
        K.stack = _tmp
        xbs = [K.sb([128, D], BF16, name=f"xb{i}") for i in range(2)]
        for ti in range((T + 127) // 128):
            n = min(128, T)
            t0 = min(ti * 128, max(T - 128, 0))
            xb = xbs[ti % 2]
            K.dma("pool", xb[0:n, :], X[t0:t0 + n, :])
            for g in range(KT // 4):
                pt = K.tbank()
                for j in range(4):
                    kt = g * 4 + j
                    K.tr(pt[:, j * 128:j * 128 + n], xb[0:n, kt * 128:(kt + 1) * 128], K.identb[0:n, 0:n])
                for j in range(4):
                    kt = g * 4 + j
                    K.copy(xT[:, kt, pad + t0: pad + t0 + n], pt[:, j * 128:j * 128 + n],
                           eng=("act" if j % 2 else "dve"))
        K.emit()
    K.stack = _old
    return xT


def setup_common(K, cst_ap):
    CS = Buf(K, cst_ap, "cst", "dram")
    K.cf = K.sb([C, CST.shape[1]], F32, name="cstf")
    K.dma("sp", K.cf[:], CS[:])
    K.init_psum(6)
    K.tbanks = [K.ps([128, 1024], BF16, name=f"tbank{i}") for i in range(2)]
    K.tb_i = 0

    def tbank():
        b = K.tbanks[K.tb_i % 2]
        K.tb_i += 1
        return b
    K.tbank = tbank
    K.identb = K.sb([128, 128], BF16, name="identb")
    K.memset(K.identb[:], 0.0)
    K.copy(K.identb[0:64, 0:64], K.cf[:, COFF["ident"]:COFF["ident"] + 64])
    K.identf = K.sb([C, C], F32, name="identf")
    K.copy(K.identf[:], K.cf[:, COFF["ident"]:COFF["ident"] + 64])
    K.dma("sp", K.identb[64:128, 64:128], K.identb[0:64, 0:64])
    K.cb = K.sb([C, CST.shape[1]], BF16, name="cstb")
    K.copy(K.cb[:], K.cf[:])

    def cc(name, w=C, f32=True):
        t = K.cf if f32 else K.cb
        return t[:, COFF[name]:COFF[name] + w]
    K.cc = cc


def rstd_from_ss(K, rstd, ss, n, eps):
    K.ts(rstd, ss, 1.0 / n, eps, ALU.mult, ALU.add)
    K.act(rstd, rstd, AF.Sqrt)
    K.recip(rstd, rstd)


def gla_chunk(K, st, q, k, g, vb, Kd, Vd):
    w = st["w"]
    psG = K.bank()
    K.mm(psG[0:C, 0:Kd], K.cc("mrefT"), g)
    ep, en = w["ep"], w["en"]
    K.act(ep[:, 0:Kd], psG[0:C, 0:Kd], AF.Exp)
    K.act(en[:, 0:Kd], psG[0:C, 0:Kd], AF.Exp, scale=-1.0)
    qs, ks = w["qs"], w["ks"]
    K.tt(qs[:, 0:Kd], q, ep[:, 0:Kd], ALU.mult)
    K.tt(ks[:, 0:Kd], k, en[:, 0:Kd], ALU.mult)
    psD = K.bank()
    K.mm(psD[0:Kd, 0:2], g, K.cc("refc", 2))
    eD = w["eD"]
    K.act(eD[0:Kd, 0:2], psD[0:Kd, 0:2], AF.Exp)
    K.tt(eD[0:Kd, 2:3], eD[0:Kd, 0:1], eD[0:Kd, 1:2], ALU.mult)
    pt = K.tbank()
    K.tr(pt[0:Kd, 0:C], qs[:, 0:Kd], K.identb[0:C, 0:C])
    K.tr(pt[0:Kd, C:2 * C], ks[:, 0:Kd], K.identb[0:C, 0:C])
    qkT = w["qkT"]
    K.copy(qkT[0:Kd, 0:2 * C], pt[0:Kd, 0:2 * C], eng="act")
    qgT = w["qgT"]
    K.ts(qgT[0:Kd, 0:C], qkT[0:Kd, 0:C], eD[0:Kd, 0:1], None, ALU.mult)
    psA = K.bank()
    K.mm(psA[0:C, 0:C], qkT[0:Kd, C:2 * C], qkT[0:Kd, 0:C])
    AT = w["AT"]
    K.tt(AT[:, 0:C], psA[0:C, 0:C], K.cc("mUi"), ALU.mult)
    o_ps = K.bank()[0:C, 0:Vd]
    K.mm(o_ps, AT[:, 0:C], vb, start=True, stop=False)
    K.mm(o_ps, qgT[0:Kd, 0:C], st["Pb"][0:Kd, 0:Vd], start=False, stop=True)
    psS = K.bank()
    K.mm(psS[0:Kd, 0:Vd], ks[:, 0:Kd], vb)
    P = st["P"]
    K.ts(P[0:Kd, 0:Vd], P[0:Kd, 0:Vd], eD[0:Kd, 2:3], None, ALU.mult)
    K.stt(P[0:Kd, 0:Vd], psS[0:Kd, 0:Vd], eD[0:Kd, 1:2], P[0:Kd, 0:Vd], ALU.mult, ALU.add)
    K.copy(st["Pb"][0:Kd, 0:Vd], P[0:Kd, 0:Vd], eng="act")
    return o_ps


def gla_work(K):
    w = {}
    w["ep"] = K.sb([C, 128], F32); w["en"] = K.sb([C, 128], F32)
    w["qs"] = K.sb([C, 128], BF16); w["ks"] = K.sb([C, 128], BF16)
    w["eD"] = K.sb([128, 4], F32)
    w["qkT"] = K.sb([128, 2 * C], BF16); w["qgT"] = K.sb([128, C], BF16)
    w["AT"] = K.sb([C, C], BF16)
    return w


def post_norm_gate(K, o_ps, Vd, normw, gate, out_bf, eps, wk):
    K.act(wk["sq"][:, 0:Vd], o_ps, AF.Square, accum=wk["ss"][:, 0:1])
    rstd_from_ss(K, wk["ss"][:, 1:2], wk["ss"][:, 0:1], Vd, eps)
    K.tt(wk["gw"][:, 0:Vd], gate, normw, ALU.mult)
    K.stt(out_bf, o_ps, wk["ss"][:, 1:2], wk["gw"][:, 0:Vd], ALU.mult, ALU.mult)


def build_mixer0(T, n_hg=4, n_gl=2):
    nc = bass.Bass("TRN2", target_bir_lowering=False)
    NW = n_hg * 512 + n_gl * 768 + 16
    HW = n_hg * 128
    GW = n_gl * 128
    NP = 3 * HW + HW + GW + 2 * GW + GW
    NO = n_hg * 128 + n_gl * 256
    x = nc.dram_tensor("x", [T, D], F32, kind="ExternalInput").ap()
    w = nc.dram_tensor("w", [D, NW], F32, kind="ExternalInput").ap()
    prm = nc.dram_tensor("prm", [C, NP], F32, kind="ExternalInput").ap()
    cst = nc.dram_tensor("cst", [C, CST.shape[1]], F32, kind="ExternalInput").ap()
    o = nc.dram_tensor("o", [T, NO], BF16, kind="ExternalOutput").ap()
    NCH = T // C
    with ExitStack() as st:
        K = KB(nc, st)
        setup_common(K, cst)
        X = Buf(K, x, "x", "dram"); W = Buf(K, w, "w", "dram"); PR = Buf(K, prm, "prm", "dram"); O = Buf(K, o, "o", "dram")
        pf = K.sb([C, NP], F32, name="prm_sb")
        K.dma("sp", pf[:], PR[:])
        o_lbl = 0; o_hn = 3 * HW; o_gb = o_hn + HW; o_gn = o_gb + GW; o_w2 = o_gn + 2 * GW
        lbe = K.sb([C, 3 * HW], F32); lb = K.sb([C, HW], F32); oml = K.sb([C, HW], F32)
        K.act(lbe[:], pf[:, 0:3 * HW], AF.Exp)
        K.tt(lb[:], lbe[:, 0:HW], lbe[:, HW:2 * HW], ALU.add)
        K.tt(lb[:], lb[:], lbe[:, 2 * HW:3 * HW], ALU.add)
        K.recip(lb[:], lb[:])
        K.tt(lb[:], lb[:], lbe[:, 0:HW], ALU.mult)
        K.ts(oml[:], lb[:], -1.0, 1.0, ALU.mult, ALU.add)
        w2b = K.sb([128, GW], BF16)
        K.memset(w2b[:], 0.0)
        K.copy(w2b[0:16, :], pf[0:16, o_w2:o_w2 + GW])
        xT = load_xT(K, X, T)
        wk = gla_work(K)
        pw = {"sq": K.sb([C, 256], F32), "ss": K.sb([C, 2], F32), "gw": K.sb([C, 256], F32)}
        P = K.sb([128, 256], F32, name="P"); Pb = K.sb([128, 256], BF16, name="Pb")
        stt = {"P": P, "Pb": Pb, "w": wk}
        wbs = [K.sb([128, KT, 768], BF16, name=f"wb{i}") for i in range(2)]
        obs = [K.sb([C, 256], BF16, name=f"ob{i}") for i in range(2)]
        f_ = K.sb([C, 128], F32); g_ = K.sb([C, 128], F32); k_ = K.sb([C, 128], F32); q_ = K.sb([C, 128], F32)
        vb_ = K.sb([C, 256], BF16); gate_ = K.sb([C, 256], F32)
        oi = 0
        _stop = 9
        if _stop <= 0:
            n_hg = 0; n_gl = 0
        if _stop == 1:
            n_gl = 0
        if _stop == 2:
            n_gl = 0; n_hg = 1
        for hh in range(n_hg):
            wb = wbs[hh % 2]
            K.dma("pool", View(wb, wb.h[:, :, 0:512]),
                  View(W, w[:, hh * 512:(hh + 1) * 512].rearrange("(kt p) n -> p kt n", p=128)))
            K.memset(P[:], 0.0); K.memset(Pb[:], 0.0)
            for n in range(NCH):
                if RST and n % RST == 0:
                    K.reset()
                ps = K.bank()
                for kt in range(KT):
                    K.mm(ps[0:C, 0:512], xT[:, kt, n * C:(n + 1) * C], wb[:, kt, 0:512], start=(kt == 0), stop=(kt == KT - 1))
                hsl = slice(hh * 128, (hh + 1) * 128)
                K.act(f_[:], ps[0:C, 128:256], AF.Sigmoid)
                K.tt(f_[:], f_[:], oml[:, hsl], ALU.mult)
                K.tt(f_[:], f_[:], lb[:, hsl], ALU.add)
                K.act(g_[:], f_[:], AF.Ln)
                K.ts(k_[:], f_[:], -1.0, 1.0, ALU.mult, ALU.add)
                K.act(q_[:], ps[0:C, 0:128], AF.Silu)
                K.copy(vb_[:, 0:128], ps[0:C, 256:384], eng="act")
                K.act(gate_[:, 0:128], ps[0:C, 384:512], AF.Silu)
                pov = gla_chunk(K, stt, q_[:], k_[:], g_[:], vb_[:, 0:128], 128, 128)
                ob = obs[oi % 2]; oi += 1
                post_norm_gate(K, pov, 128, pf[:, o_hn + hh * 128:o_hn + (hh + 1) * 128], gate_[:, 0:128],
                               ob[:, 0:128], 1e-6, pw)
                K.dma("sp", O[n * C:(n + 1) * C, hh * 128:(hh + 1) * 128], ob[:, 0:128])
        wgd = K.sb([128, KT, 128], BF16, name="wgd")
        c_gd = 4 * 512 + 2 * 768
        K.memset(wgd[:], 0.0)
        K.dma("pool", View(wgd, wgd.h[:, :, 0:16]), View(W, w[:, c_gd:c_gd + 16].rearrange("(kt p) n -> p kt n", p=128)))
        gdT = K.sb([128, C], BF16); sp_ = K.sb([C, 128], F32)
        for gg in range(n_gl):
            wb = wbs[(n_hg + gg) % 2]
            c0 = n_hg * 512 + gg * 768
            K.dma("pool", View(wb, wb.h[:, :, 0:768]),
                  View(W, w[:, c0:c0 + 768].rearrange("(kt p) n -> p kt n", p=128)))
            K.memset(P[:], 0.0); K.memset(Pb[:], 0.0)
            for n in range(NCH):
                if RST and n % RST == 0:
                    K.reset()
                ps = K.bank(); ps2 = K.bank(); ps3 = K.bank()
                for kt in range(KT):
                    K.mm(ps[0:C, 0:512], xT[:, kt, n * C:(n + 1) * C], wb[:, kt, 0:512], start=(kt == 0), stop=(kt == KT - 1))
                for kt in range(KT):
                    K.mm(ps2[0:C, 0:256], xT[:, kt, n * C:(n + 1) * C], wb[:, kt, 512:768], start=(kt == 0), stop=(kt == KT - 1))
                for kt in range(KT):
                    K.mm(ps3[0:128, 0:C], wgd[:, kt, :], xT[:, kt, n * C:(n + 1) * C], start=(kt == 0), stop=(kt == KT - 1))
                K.copy(gdT[:], ps3[0:128, 0:C])
                pz = K.bank()
                K.mm(pz[0:C, 0:128], gdT[:], w2b[:, gg * 128:(gg + 1) * 128])
                K.tt(sp_[:], pz[0:C, 0:128], pf[:, o_gb + gg * 128:o_gb + (gg + 1) * 128], ALU.add)
                K.act(sp_[:], sp_[:], AF.Exp, scale=-1.0)
                K.ts(sp_[:], sp_[:], 1.0, None, ALU.add)
                K.act(sp_[:], sp_[:], AF.Ln)
                K.ts(g_[:], sp_[:], -1.0 / 16.0, None, ALU.mult)
                K.ts(q_[:], ps[0:C, 0:128], 128 ** -0.5, None, ALU.mult)
                K.copy(k_[:], ps[0:C, 128:256])
                K.copy(vb_[:], ps[0:C, 256:512], eng="act")
                K.act(gate_[:], ps2[0:C, 0:256], AF.Silu)
                pov = gla_chunk(K, stt, q_[:], k_[:], g_[:], vb_[:], 128, 256)
                ob = obs[oi % 2]; oi += 1
                post_norm_gate(K, pov, 256, pf[:, o_gn + gg * 256:o_gn + (gg + 1) * 256], gate_[:],
                               ob[:], 1e-6, pw)
                K.dma("sp", O[n * C:(n + 1) * C, HW + gg * 256:HW + (gg + 1) * 256], ob[:])
        if _stop <= 0:
            K.dma("sp", O[0:C, 0:256], obs[0][:])
        K.finish([O] + obs)
        K.emit()
    return nc


def extra_consts():
    sh = np.zeros((C, C), np.float32)
    for t in range(1, C):
        sh[t - 1, t] = 1.0
    bs = np.zeros((C, C), np.float32)
    bs[C - 1, 0] = 1.0
    return {"shiftT": sh, "bselT": bs}

_ex = extra_consts()
for _i, (_k, _v) in enumerate(_ex.items()):
    COFF[_k] = CST.shape[1] + _i * C
CST = np.concatenate([CST] + list(_ex.values()), axis=1)


def neumann(K, w, M, MT, n):
    X = w["X"]
    K.tt(X[0:n, 0:n], M, K.identf[0:n, 0:n], ALU.add)
    A, AT = M, MT
    for it in range(5):
        pa = K.bank()
        K.mm(pa[0:n, 0:n], AT, A)
        K.mm(pa[0:n, n:2 * n], A, AT)
        A2 = w["A"][it % 2]
        K.copy(A2[0:n, 0:2 * n], pa[0:n, 0:2 * n], eng="act")
        px = K.bank()
        K.mm(px[0:n, 0:n], A2[0:n, n:2 * n], X[0:n, 0:n])
        K.tt(X[0:n, 0:n], X[0:n, 0:n], px[0:n, 0:n], ALU.add)
        A, AT = A2[0:n, 0:n], A2[0:n, n:2 * n]
    return X[0:n, 0:n]


def rwkv_chunk(K, w, st, r, k, v, wlog, av, bv):
    psG = K.bank()
    K.mm(psG[0:C, 0:C], K.cc("mrefT"), wlog)
    ep, en, ew = w["ep"], w["en"], w["ew"]
    K.act(ep[:], psG[0:C, 0:C], AF.Exp)
    K.act(en[:], psG[0:C, 0:C], AF.Exp, scale=-1.0)
    K.act(ew[:], wlog, AF.Exp, scale=-1.0)
    K.tt(ew[:], ew[:], ep[:], ALU.mult)
    S4 = w["S4"]
    K.tt(S4[:, 0:C], r, ep[:], ALU.mult)
    K.tt(S4[:, C:2 * C], av, ew[:], ALU.mult)
    K.tt(S4[:, 2 * C:3 * C], k, en[:], ALU.mult)
    K.tt(S4[:, 3 * C:4 * C], bv, en[:], ALU.mult)
    psD = K.bank()
    K.mm(psD[0:C, 0:2], wlog, K.cc("refc", 2))
    eD = w["eD"]
    K.act(eD[:, 0:2], psD[0:C, 0:2], AF.Exp)
    K.tt(eD[:, 2:3], eD[:, 0:1], eD[:, 1:2], ALU.mult)
    pt = K.tbank()
    for i in range(4):
        K.tr(pt[0:C, i * C:(i + 1) * C], S4[:, i * C:(i + 1) * C], K.identb[0:C, 0:C])
    T4 = w["T4"]
    K.copy(T4[:], pt[0:C, 0:4 * C], eng="act")
    G2 = w["G2"]
    K.ts(G2[:, 0:2 * C], T4[:, 0:2 * C], eD[:, 0:1], None, ALU.mult)
    psA = K.bank()
    K.mm(psA[0:C, 0:2 * C], T4[:, 2 * C:3 * C], T4[:, 0:2 * C])
    K.mm(psA[0:C, 2 * C:4 * C], T4[:, 3 * C:4 * C], T4[:, 0:2 * C])
    K.mm(psA[0:C, 4 * C:5 * C], T4[:, C:2 * C], T4[:, 3 * C:4 * C])
    AM = w["AM"]
    K.tt(AM[:, 0:C], psA[0:C, 0:C], K.cc("mUi"), ALU.mult)
    K.tt(AM[:, C:2 * C], psA[0:C, C:2 * C], K.cc("mUs"), ALU.mult)
    K.tt(AM[:, 2 * C:3 * C], psA[0:C, 2 * C:3 * C], K.cc("mUi"), ALU.mult)
    Mf = w["Mf"]
    K.tt(Mf[:, 0:C], psA[0:C, 3 * C:4 * C], K.cc("mUs"), ALU.mult)
    K.tt(Mf[:, C:2 * C], psA[0:C, 4 * C:5 * C], K.cc("mLs"), ALU.mult)
    X = neumann(K, w, Mf[:, 0:C], Mf[:, C:2 * C], C)
    Pb = st["Pb"]
    psR = K.bank()
    K.mm(psR[0:C, 0:C], AM[:, C:2 * C], v["b"], start=True, stop=False)
    K.mm(psR[0:C, 0:C], G2[:, C:2 * C], Pb[:], start=False, stop=True)
    RH = w["RH"]
    K.copy(RH[:], psR[0:C, 0:C], eng="act")
    psU = K.bank()
    K.mm(psU[0:C, 0:C], X, RH[:])
    Ub = w["Ub"]
    K.copy(Ub[:], psU[0:C, 0:C], eng="act")
    y_ps = K.bank()[0:C, 0:C]
    K.mm(y_ps, AM[:, 0:C], v["b"], start=True, stop=False)
    K.mm(y_ps, AM[:, 2 * C:3 * C], Ub[:], start=False, stop=False)
    K.mm(y_ps, G2[:, 0:C], Pb[:], start=False, stop=True)
    psS = K.bank()
    K.mm(psS[0:C, 0:C], S4[:, 2 * C:3 * C], v["b"], start=True, stop=False)
    K.mm(psS[0:C, 0:C], S4[:, 3 * C:4 * C], Ub[:], start=False, stop=True)
    P = st["P"]
    K.ts(P[:], P[:], eD[:, 2:3], None, ALU.mult)
    K.stt(P[:], psS[0:C, 0:C], eD[:, 1:2], P[:], ALU.mult, ALU.add)
    K.copy(Pb[:], P[:], eng="act")
    return y_ps


def gdn_chunk(K, w, st, q, k, v, g, beta):
    K.copy(w["g2"][:, 0:1], g)
    K.copy(w["g2"][:, 1:2], beta)
    psG = K.bank()
    K.mm(psG[0:C, 0:2], K.cc("triT"), w["g2"][:, 0:2])
    sc = w["sc"]
    K.copy(sc[:, 0:1], psG[0:C, 0:1])
    gB = w["gB"]
    K.ts(gB[:], w["ones128"][:], g, None, ALU.mult)
    psRow = K.bank()
    K.mm(psRow[0:C, 0:C], gB[:, 0:C], K.cc("triT"))
    K.mm(psRow[0:128, C:C + 2], gB[:, 0:128], K.cc("ones", 2))
    GlB = w["GlB"]
    K.copy(GlB[:, 0:1], psRow[0:128, C:C + 1])
    K.act(GlB[:, 1:2], GlB[:, 0:1], AF.Exp)
    dd = w["dd"]
    K.ts(dd[:], psRow[0:C, 0:C], sc[:, 0:1], 0.0, ALU.subtract, ALU.min)
    K.act(dd[:], dd[:], AF.Exp)
    gi, gs = w["gi"], w["gs"]
    K.tt(gi[:], dd[:], K.cc("mUi"), ALU.mult)
    K.tt(gs[:], dd[:], K.cc("mUs"), ALU.mult)
    K.act(sc[:, 1:2], sc[:, 0:1], AF.Exp)
    K.act(sc[:, 2:3], sc[:, 0:1], AF.Exp, scale=-1.0, bias=GlB[0:C, 0:1])
    B4 = w["B4"]
    kbg, vbt, kg = w["kbg"], w["vbt"], w["kg"]
    K.copy(B4[:, 0:128], k)
    K.ts(B4[:, 128:256], k, beta, None, ALU.mult)
    K.copy(B4[:, 256:384], q)
    K.ts(B4[:, 384:512], q, sc[:, 1:2], None, ALU.mult)
    K.ts(kbg[:], k, beta, sc[:, 1:2], ALU.mult, ALU.mult)
    K.ts(vbt[:], v, beta, None, ALU.mult)
    K.ts(kg[:], k, sc[:, 2:3], None, ALU.mult)
    pt = K.tbank()
    for i in range(4):
        K.tr(pt[:, i * C:(i + 1) * C], B4[:, i * 128:(i + 1) * 128], K.identb[0:C, 0:C])
    T4 = w["T4g"]
    K.copy(T4[:], pt[:, 0:4 * C], eng="act")
    psL = K.bank()
    K.mm(psL[0:C, 0:2 * C], T4[:, 0:C], T4[:, C:3 * C])
    Mf = w["Mf"]
    K.stt(Mf[:, 0:C], psL[0:C, 0:C], -1.0, gs[:], ALU.mult, ALU.mult)
    aq = w["aq"]
    K.tt(aq[:], psL[0:C, C:2 * C], gi[:], ALU.mult)
    ptm = K.bank()
    K.tr(ptm[0:C, 0:C], Mf[:, 0:C], K.identf[:])
    K.copy(Mf[:, C:2 * C], ptm[0:C, 0:C], eng="act")
    X = neumann(K, w, Mf[:, 0:C], Mf[:, C:2 * C], C)
    psU = K.bank()
    K.mm(psU[0:C, 0:128], X, vbt[:])
    u = w["u"]
    K.copy(u[:], psU[0:C, 0:128], eng="act")
    psWT = K.bank()
    K.mm(psWT[0:128, 0:C], kbg[:], X)
    wT = w["wT"]
    K.copy(wT[:], psWT[0:128, 0:C], eng="act")
    Sb = st["Sb"]
    psW = K.bank()
    K.mm(psW[0:C, 0:128], wT[:], Sb[:])
    vn = w["vn"]
    K.tt(vn[:], u[:], psW[0:C, 0:128], ALU.subtract)
    o_ps = K.bank()[0:C, 0:128]
    K.mm(o_ps, T4[:, 3 * C:4 * C], Sb[:], start=True, stop=False)
    K.mm(o_ps, aq[:], vn[:], start=False, stop=True)
    psS = K.bank()
    K.mm(psS[0:128, 0:128], kg[:], vn[:])
    S = st["S"]
    K.stt(S[:], S[:], GlB[:, 1:2], psS[0:128, 0:128], ALU.mult, ALU.add)
    K.copy(Sb[:], S[:], eng="act")
    return o_ps


NR = 8
NG = 4
RW_COLS = 3 * NR * 64 + 96 + 96 + 256
GH_COLS = 384 + 130


def build_mixer1(T, n_r=NR, n_g=NG):
    nc = bass.Bass("TRN2", target_bir_lowering=False)
    RC = n_r * 64
    RWC = 3 * RC + 448
    x = nc.dram_tensor("x", [T, D], F32, kind="ExternalInput").ap()
    wr = nc.dram_tensor("wr", [D, RWC], F32, kind="ExternalInput").ap()
    wg = nc.dram_tensor("wg", [D, n_g * GH_COLS], F32, kind="ExternalInput").ap()
    NPR = RWC + 7 * RC + n_g * (4 * 384 + 2 + 128)
    prm = nc.dram_tensor("prm", [128, NPR], F32, kind="ExternalInput").ap()
    lora = nc.dram_tensor("lora", [128, 4 * RC], F32, kind="ExternalInput").ap()
    cst = nc.dram_tensor("cst", [C, CST.shape[1]], F32, kind="ExternalInput").ap()
    NO = RC + n_g * 128
    o = nc.dram_tensor("o", [T, NO], BF16, kind="ExternalOutput").ap()
    NCH = T // C
    PAD = 4
    with ExitStack() as st:
        K = KB(nc, st)
        setup_common(K, cst)
        X = Buf(K, x, "x", "dram"); WR = Buf(K, wr, "wr", "dram"); WG = Buf(K, wg, "wg", "dram")
        PR = Buf(K, prm, "prm", "dram"); LO = Buf(K, lora, "lora", "dram"); O = Buf(K, o, "o", "dram")
        xT = load_xT(K, X, T, pad=PAD)
        obs = [K.sb([C, 128], BF16, name=f"ob{i}") for i in range(2)]
        oi = 0
        wk = {"X": K.sb([C, C], F32), "A": [K.sb([C, 2 * C], F32) for _ in range(2)], "Mf": K.sb([C, 2 * C], F32),
              "eD": K.sb([C, 4], F32)}
        sq = K.sb([C, 128], F32); ssb = K.sb([C, 8], F32)
        with ExitStack() as st2:
            K.stack = st2
            pf = K.sb([C, RWC + 7 * RC], F32, name="prm_r")
            K.dma("sp", pf[:], PR[0:C, 0:RWC + 7 * RC])
            o_mu = 0; o_w0 = RWC; o_a0 = o_w0 + RC; o_kk = o_a0 + RC; o_ka = o_kk + RC; o_rk = o_ka + RC
            o_gw = o_rk + RC; o_gb = o_gw + RC
            lb16 = K.sb([128, 4 * RC], BF16, name="lora_b")
            with ExitStack() as _tmp:
                K.stack = _tmp
                lf = K.sb([128, 4 * RC], F32, name="lora_f")
                K.dma("sp", lf[:], LO[:])
                K.copy(lb16[:], lf[:])
                K.emit()
            K.stack = st2
            wrb = K.sb([128, KT, RWC], BF16, name="wrb")
            for c0 in range(0, RWC, 512):
                cw = min(512, RWC - c0)
                K.dma("pool", View(wrb, wrb.h[:, :, c0:c0 + cw]),
                      View(WR, wr[:, c0:c0 + cw].rearrange("(kt p) n -> p kt n", p=128)))
            prs = [K.sb([C, RWC], F32, name=f"pr{i}") for i in range(2)]
            pm = K.sb([C, RWC], F32, name="pm")
            w = dict(wk)
            w.update({"ep": K.sb([C, C], F32), "en": K.sb([C, C], F32), "ew": K.sb([C, C], F32),
                      "S4": K.sb([C, 4 * C], BF16), "T4": K.sb([C, 4 * C], BF16), "G2": K.sb([C, 2 * C], BF16),
                      "AM": K.sb([C, 3 * C], BF16), "RH": K.sb([C, C], F32), "Ub": K.sb([C, C], BF16)})
            Ps = [K.sb([C, C], F32, name=f"P{h}") for h in range(n_r)]
            Pbs = [K.sb([C, C], BF16, name=f"Pb{h}") for h in range(n_r)]
            for h in range(n_r):
                K.memset(Ps[h][:], 0.0); K.memset(Pbs[h][:], 0.0)
            lor = K.sb([C, 448], BF16, name="lor")
            lT = K.sb([128, 4 * C], BF16, name="lT")
            K.memset(lT[:], 0.0)
            wl = K.sb([C, RC], F32); aa = K.sb([C, RC], F32); gt = K.sb([C, RC], F32)
            kkb = K.sb([C, RC], F32); kf = K.sb([C, RC], F32); bvv = K.sb([C, RC], F32); avv = K.sb([C, RC], F32)
            vb = K.sb([C, RC], BF16); tmp = avv; bon = K.sb([C, 8], F32)
            yc = K.sb([C, C], F32)
            for n in range(NCH):
                if RST:
                    K.reset()
                pr = prs[n % 2]
                for c0 in range(0, RWC, 512):
                    cw = min(512, RWC - c0)
                    ps = K.bank()
                    for kt in range(KT):
                        K.mm(ps[0:C, 0:cw], xT[:, kt, PAD + n * C:PAD + (n + 1) * C], wrb[:, kt, c0:c0 + cw],
                             start=(kt == 0), stop=(kt == KT - 1))
                    K.copy(pr[:, c0:c0 + cw], ps[0:C, 0:cw], eng="act")
                for c0 in range(0, RWC, 512):
                    cw = min(512, RWC - c0)
                    pv = K.bank()
                    K.mm(pv[0:C, 0:cw], K.cc("shiftT"), pr[:, c0:c0 + cw], start=True, stop=(n == 0))
                    if n > 0:
                        K.mm(pv[0:C, 0:cw], K.cc("bselT"), prs[(n - 1) % 2][:, c0:c0 + cw], start=False, stop=True)
                    K.tt(pm[:, c0:c0 + cw], pv[0:C, 0:cw], pr[:, c0:c0 + cw], ALU.subtract)
                    K.tt(pm[:, c0:c0 + cw], pm[:, c0:c0 + cw], pf[:, o_mu + c0:o_mu + c0 + cw], ALU.mult)
                    K.tt(pm[:, c0:c0 + cw], pm[:, c0:c0 + cw], pr[:, c0:c0 + cw], ALU.add)
                r_ = lambda h: pm[:, h * C:(h + 1) * C]
                kr_ = lambda h: pm[:, RC + h * C:RC + (h + 1) * C]
                cL = 3 * RC
                K.act(lor[:, 0:96], pm[:, cL:cL + 96], AF.Tanh)
                K.copy(lor[:, 96:192], pm[:, cL + 96:cL + 192])
                K.act(lor[:, 192:448], pm[:, cL + 192:cL + 448], AF.Sigmoid)
                pt = K.tbank()
                K.tr(pt[0:96, 0:C], lor[:, 0:96], K.identb[0:C, 0:C])
                K.tr(pt[0:96, C:2 * C], lor[:, 96:192], K.identb[0:C, 0:C])
                K.tr(pt[0:128, 2 * C:3 * C], lor[:, 192:320], K.identb[0:C, 0:C])
                K.tr(pt[0:128, 3 * C:4 * C], lor[:, 320:448], K.identb[0:C, 0:C])
                K.copy(lT[0:96, 0:2 * C], pt[0:96, 0:2 * C], eng="act")
                K.copy(lT[:, 2 * C:4 * C], pt[:, 2 * C:4 * C], eng="act")
                pz = K.bank()
                K.mm(pz[0:C, 0:RC], lT[:, 0:C], lb16[:, 0:RC])
                K.tt(wl[:], pz[0:C, 0:RC], pf[:, o_w0:o_w0 + RC], ALU.add)
                K.act(wl[:], wl[:], AF.Sigmoid)
                K.ts(wl[:], wl[:], -float(np.exp(-0.5)), None, ALU.mult)
                pz2 = K.bank()
                K.mm(pz2[0:C, 0:RC], lT[:, C:2 * C], lb16[:, RC:2 * RC])
                K.tt(aa[:], pz2[0:C, 0:RC], pf[:, o_a0:o_a0 + RC], ALU.add)
                K.act(aa[:], aa[:], AF.Sigmoid)
                pz3 = K.bank()
                K.mm(pz3[0:C, 0:RC], lT[:, 2 * C:3 * C], lb16[:, 2 * RC:3 * RC], start=True, stop=False)
                K.mm(pz3[0:C, 0:RC], lT[:, 3 * C:4 * C], lb16[:, 3 * RC:4 * RC], start=False, stop=True)
                K.copy(gt[:], pz3[0:C, 0:RC], eng="act")
                K.tt(kkb[:], pm[:, RC:2 * RC], pf[:, o_kk:o_kk + RC], ALU.mult)
                K.tt(tmp[:], kkb[:], kkb[:], ALU.mult)
                K.reduce(ssb[:, 0:n_r], View(tmp, tmp.h[:, :].rearrange("p (h c) -> p h c", c=C)))
                K.ts(ssb[:, 0:n_r], ssb[:, 0:n_r], 1e-6, None, ALU.add)
                K.act(ssb[:, 0:n_r], ssb[:, 0:n_r], AF.Sqrt)
                K.recip(ssb[:, 0:n_r], ssb[:, 0:n_r])
                K.ts(kf[:], aa[:], -1.0, None, ALU.add)
                K.tt(kf[:], kf[:], pf[:, o_ka:o_ka + RC], ALU.mult)
                K.ts(kf[:], kf[:], 1.0, None, ALU.add)
                K.tt(kf[:], kf[:], pm[:, RC:2 * RC], ALU.mult)
                K.copy(vb[:], pm[:, 2 * RC:3 * RC])
                K.tt(tmp[:], pm[:, 0:RC], kf[:], ALU.mult)
                K.tt(tmp[:], tmp[:], pf[:, o_rk:o_rk + RC], ALU.mult)
                K.reduce(bon[:, 0:n_r], View(tmp, tmp.h[:, :].rearrange("p (h c) -> p h c", c=C)))
                for h in range(n_r):
                    hs = slice(h * C, (h + 1) * C)
                    K.ts(kkb[:, hs], kkb[:, hs], ssb[:, h:h + 1], None, ALU.mult)
                    K.tt(bvv[:, hs], kkb[:, hs], aa[:, hs], ALU.mult)
                    K.ts(avv[:, hs], kkb[:, hs], -1.0, None, ALU.mult)
                    pyv = rwkv_chunk(K, w, {"P": Ps[h], "Pb": Pbs[h]}, r_(h), kf[:, hs], {"b": vb[:, hs]}, wl[:, hs],
                                     avv[:, hs], bvv[:, hs])
                    K.reduce(sq[:, 0:1], pyv)
                    K.ts(sq[:, 0:1], sq[:, 0:1], 1.0 / C, None, ALU.mult)
                    K.ts(yc[:], pyv, sq[:, 0:1], None, ALU.subtract)
                    K.act(sq[:, 64:128], yc[:], AF.Square, accum=sq[:, 1:2])
                    rstd_from_ss(K, sq[:, 2:3], sq[:, 1:2], C, 64e-5)
                    K.stt(yc[:], yc[:], sq[:, 2:3], pf[:, o_gw + h * C:o_gw + (h + 1) * C], ALU.mult, ALU.mult)
                    K.tt(yc[:], yc[:], pf[:, o_gb + h * C:o_gb + (h + 1) * C], ALU.add)
                    K.stt(yc[:], pm[:, 2 * RC + h * C:2 * RC + (h + 1) * C], bon[:, h:h + 1], yc[:], ALU.mult, ALU.add)
                    ob = obs[oi % 2]; oi += 1
                    K.tt(ob[:, 0:C], yc[:], gt[:, hs], ALU.mult)
                    K.dma("sp", O[n * C:(n + 1) * C, h * C:(h + 1) * C], ob[:, 0:C])
            K.finish(obs)
            K.emit()
        K.stack = st
        with ExitStack() as st3:
            K.stack = st3
            o_g0 = RWC + 7 * RC
            GP = 4 * 384 + 2 + 128
            w = dict(wk)
            w.update({"sc": K.sb([C, 8], F32), "g2": K.sb([C, 2], F32), "gB": K.sb([C, 128], F32), "ones128": K.sb([C, 128], F32),
                      "GlB": K.sb([128, 2], F32), "dd": K.sb([C, C], F32), "gi": K.sb([C, C], F32), "gs": K.sb([C, C], F32),
                      "B4": K.sb([C, 512], BF16), "kbg": K.sb([C, 128], F32), "vbt": K.sb([C, 128], F32),
                      "kg": K.sb([C, 128], BF16), "T4g": K.sb([128, 4 * C], BF16), "aq": K.sb([C, C], BF16),
                      "u": K.sb([C, 128], F32), "wT": K.sb([128, C], BF16), "vn": K.sb([C, 128], BF16)})
            K.memset(w["ones128"][:], 1.0)
            S = K.sb([128, 128], F32, name="S"); Sb = K.sb([128, 128], BF16, name="Sb")
            wq = K.sb([128, KT, 384], BF16, name="wq")
            wcs = [K.sb([128, KT, 384], BF16, name=f"wc{j}") for j in range(4)]
            wz = K.sb([128, KT, 130], BF16, name="wz")
            pg = K.sb([128, GP], F32, name="prm_g")
            qkv = K.sb([C, 384], F32); zg = K.sb([C, 128], F32); ab = K.sb([C, 4], F32); gw = K.sb([C, 128], F32)
            qn = K.sb([C, 128], F32); kn = K.sb([C, 128], F32)
            for hd in range(n_g):
                c0 = hd * GH_COLS
                K.dma("sp", pg[:], PR[:, o_g0 + hd * GP:o_g0 + (hd + 1) * GP])
                K.dma("pool", View(wq, wq.h[:]), View(WG, wg[:, c0:c0 + 384].rearrange("(kt p) n -> p kt n", p=128)))
                K.dma("pool", View(wz, wz.h[:]), View(WG, wg[:, c0 + 384:c0 + 514].rearrange("(kt p) n -> p kt n", p=128)))
                for j in range(4):
                    for kt in range(KT):
                        K.tt(wcs[j][:, kt, :], wq[:, kt, :], pg[:, j * 384:(j + 1) * 384], ALU.mult,
                             eng="dve")
                K.act(ab[:, 2:3], pg[0:C, 1536:1537], AF.Exp)
                K.memset(S[:], 0.0); K.memset(Sb[:], 0.0)
                for n in range(NCH):
                    if RST and n % 2 == 0:
                        K.reset()
                    ps = K.bank()
                    i = 0
                    for j in range(4):
                        sh = 3 - j
                        for kt in range(KT):
                            K.mm(ps[0:C, 0:384], xT[:, kt, PAD + n * C - sh:PAD + (n + 1) * C - sh], wcs[j][:, kt, :],
                                 start=(i == 0), stop=(i == 4 * KT - 1))
                            i += 1
                    ps2 = K.bank()
                    for kt in range(KT):
                        K.mm(ps2[0:C, 0:130], xT[:, kt, PAD + n * C:PAD + (n + 1) * C], wz[:, kt, :],
                             start=(kt == 0), stop=(kt == KT - 1))
                    K.act(qkv[:], ps[0:C, 0:384], AF.Silu)
                    K.act(zg[:], ps2[0:C, 0:128], AF.Silu)
                    K.act(ab[:, 0:1], ps2[0:C, 128:129], AF.Exp, bias=pg[0:C, 1537:1538])
                    K.ts(ab[:, 0:1], ab[:, 0:1], 1.0, None, ALU.add)
                    K.act(ab[:, 0:1], ab[:, 0:1], AF.Ln)
                    K.stt(ab[:, 0:1], ab[:, 0:1], -1.0, ab[:, 2:3], ALU.mult, ALU.mult)
                    K.act(ab[:, 1:2], ps2[0:C, 129:130], AF.Sigmoid)
                    K.act(sq[:, 0:128], qkv[:, 0:128], AF.Square, accum=ssb[:, 0:1])
                    K.act(sq[:, 0:128], qkv[:, 128:256], AF.Square, accum=ssb[:, 1:2])
                    K.ts(ssb[:, 0:2], ssb[:, 0:2], 1e-6, None, ALU.add)
                    K.act(ssb[:, 0:2], ssb[:, 0:2], AF.Sqrt)
                    K.recip(ssb[:, 0:2], ssb[:, 0:2])
                    K.ts(qn[:], qkv[:, 0:128], ssb[:, 0:1], 128 ** -0.5, ALU.mult, ALU.mult)
                    K.ts(kn[:], qkv[:, 128:256], ssb[:, 1:2], None, ALU.mult)
                    pov = gdn_chunk(K, w, {"S": S, "Sb": Sb}, qn[:], kn[:], qkv[:, 256:384], ab[:, 0:1], ab[:, 1:2])
                    K.act(sq[:, 0:128], pov, AF.Square, accum=ssb[:, 2:3])
                    rstd_from_ss(K, ssb[:, 3:4], ssb[:, 2:3], 128, 1e-6)
                    K.tt(gw[:], zg[:], pg[0:C, 1538:1538 + 128], ALU.mult)
                    ob = obs[oi % 2]; oi += 1
                    K.stt(ob[:, 0:128], pov, ssb[:, 3:4], gw[:], ALU.mult, ALU.mult)
                    K.dma("sp", O[n * C:(n + 1) * C, RC + hd * 128:RC + (hd + 1) * 128], ob[:, 0:128])
            K.finish([O] + obs)
            K.emit()
        K.stack = st
    return nc


ALPHA = float((2 * 2) ** 0.25)
FF = 5632
FT = FF // 128


def layer_norm_tile(K, h, n, lnw, lnb, wk):
    K.reduce(wk["st"][0:n, 0:1], h)
    K.ts(wk["st"][0:n, 0:1], wk["st"][0:n, 0:1], 1.0 / D, None, ALU.mult)
    K.ts(h, h, wk["st"][0:n, 0:1], None, ALU.subtract)
    K.act(wk["junk"][0:n, :], h, AF.Square, accum=wk["st"][0:n, 1:2])
    rstd_from_ss(K, wk["st"][0:n, 2:3], wk["st"][0:n, 1:2], D, 1e-5)
    K.stt(h, h, wk["st"][0:n, 2:3], lnw, ALU.mult, ALU.mult)
    K.tt(h, h, lnb, ALU.add)


def build_dense(NT):
    nc = bass.Bass("TRN2", target_bir_lowering=False)
    NR = NT + 2
    o_in = nc.dram_tensor("o", [NR, D], BF16, kind="ExternalInput").ap()
    xr = nc.dram_tensor("xr", [NR, D], F32, kind="ExternalInput").ap()
    pl = nc.dram_tensor("pl", [NT, 256], F32, kind="ExternalInput").ap()
    w_out = nc.dram_tensor("w_out", [D, D], F32, kind="ExternalInput").ap()
    w_up = nc.dram_tensor("w_up", [D, 2 * FF], F32, kind="ExternalInput").ap()
    w_down = nc.dram_tensor("w_down", [FF, D], F32, kind="ExternalInput").ap()
    w_pg = nc.dram_tensor("w_pg", [D, D], F32, kind="ExternalInput").ap()
    w_pp = nc.dram_tensor("w_pp", [256, D], F32, kind="ExternalInput").ap()
    lnp = nc.dram_tensor("lnp", [128, 4 * D], F32, kind="ExternalInput").ap()
    cwp = nc.dram_tensor("cwp", [128, FT * 6 + 1], F32, kind="ExternalInput").ap()
    cst = nc.dram_tensor("cst", [C, CST.shape[1]], F32, kind="ExternalInput").ap()
    y = nc.dram_tensor("y", [NT, D], F32, kind="ExternalOutput").ap()
    x1s = nc.dram_tensor("x1s", [NR, D], F32, kind="Internal").ap()
    h2s = nc.dram_tensor("h2s", [NT, D], F32, kind="Internal").ap()
    tiles = [(min(r0, NR - 128), 128) for r0 in range(0, NR, 128)]
    otiles = [(r0, min(128, NT - r0)) for r0 in range(0, NT, 128)]
    with ExitStack() as st:
        K = KB(nc, st)
        setup_common(K, cst)
        OI = Buf(K, o_in, "o", "dram"); XR = Buf(K, xr, "xr", "dram"); PL = Buf(K, pl, "pl", "dram")
        WO = Buf(K, w_out, "w_out", "dram"); WU = Buf(K, w_up, "w_up", "dram"); WD = Buf(K, w_down, "w_down", "dram")
        WPG = Buf(K, w_pg, "w_pg", "dram"); WPP = Buf(K, w_pp, "w_pp", "dram"); LNP = Buf(K, lnp, "lnp", "dram")
        CWP = Buf(K, cwp, "cwp", "dram"); Y = Buf(K, y, "y", "dram")
        X1S = Buf(K, x1s, "x1s", "dram"); H2S = Buf(K, h2s, "h2s", "dram")
        cw = K.sb([128, FT * 6 + 1], F32, name="cw")
        K.dma("sp", cw[:], CWP[:])
        aT = load_xT(K, OI, NR)
        wk = {"st": K.sb([128, 4], F32), "junk": K.sb([128, D], F32, name="junk")}
        xbf = K.sb([128, D], BF16, name="xbf")
        with ExitStack() as sa:
            K.stack = sa
            lnw = K.sb([128, D], F32); lnb = K.sb([128, D], F32)
            K.dma("sp", lnw[:], LNP[:, 0:D]); K.dma("sp", lnb[:], LNP[:, D:2 * D])
            h = K.sb([128, len(tiles), D], F32, name="h")
            wbs = [K.sb([128, KT, 512], BF16, name=f"wo{i}") for i in range(2)]
            xts = [K.sb([128, 512], F32, name=f"xt{i}") for i in range(2)]
            xi = 0
            for cb in range(4):
                if RST:
                    K.reset()
                wb = wbs[cb % 2]
                K.dma("pool", View(wb, wb.h[:]), View(WO, w_out[:, cb * 512:(cb + 1) * 512].rearrange("(kt p) n -> p kt n", p=128)))
                for ti, (r0, n) in enumerate(tiles):
                    ps = K.bank()
                    for kt in range(KT):
                        K.mm(ps[0:n, 0:512], aT[:, kt, r0:r0 + n], wb[:, kt, :], start=(kt == 0), stop=(kt == KT - 1))
                    xt = xts[xi % 2]; xi += 1
                    K.dma("sp", xt[0:n, :], XR[r0:r0 + n, cb * 512:(cb + 1) * 512])
                    K.stt(h[0:n, ti, cb * 512:(cb + 1) * 512], xt[0:n, :], ALPHA, ps[0:n, 0:512], ALU.mult, ALU.add)
            for ti, (r0, n) in enumerate(tiles):
                if RST and ti % 4 == 0:
                    K.reset()
                hv = h[0:n, ti, :]
                layer_norm_tile(K, hv, n, lnw[0:n, :], lnb[0:n, :], wk)
                K.dma("sp", X1S[r0:r0 + n, :], hv)
                K.copy(xbf[0:n, :], hv, eng="act")
                for g in range(KT // 4):
                    pt = K.tbank()
                    for j in range(4):
                        kt = g * 4 + j
                        K.tr(pt[:, j * 128:j * 128 + n], xbf[0:n, kt * 128:(kt + 1) * 128], K.identb[0:n, 0:n])
                    for j in range(4):
                        kt = g * 4 + j
                        K.copy(aT[:, kt, r0:r0 + n], pt[:, j * 128:j * 128 + n], eng=("act" if j % 2 else "dve"))
            K.emit()
        K.stack = st
        sact = ExitStack()
        K.stack = sact
        actT = K.sb([128, FT, NT], BF16, name="actT")
        with ExitStack() as sb_:
            K.stack = sb_
            wus = [K.sb([128, KT, 512], BF16, name=f"wu{i}") for i in range(2)]
            raws = [K.sb([128, 512], F32, name=f"raw{i}") for i in range(2)]
            us = [K.sb([128, 512], F32, name=f"u{i}") for i in range(2)]
            blocks = []
            j0 = 0
            while j0 + 2 < NR:
                wdt = min(512, NR - j0)
                blocks.append((j0, wdt))
                j0 += wdt - 2
            for fp in range(FT // 2):
                if RST and fp % 2 == 0:
                    K.reset()
                wu = wus[fp % 2]
                K.dma("pool", View(wu, wu.h[:, :, 0:256]),
                      View(WU, w_up[:, fp * 256:(fp + 1) * 256].rearrange("(kt p) n -> p kt n", p=128)))
                K.dma("pool", View(wu, wu.h[:, :, 256:512]),
                      View(WU, w_up[:, FF + fp * 256:FF + (fp + 1) * 256].rearrange("(kt p) n -> p kt n", p=128)))
                for fi in range(2):
                    ft = fp * 2 + fi
                    for (b0, wdt) in blocks:
                        for gv in range(2):
                            ps = K.bank()
                            for kt in range(KT):
                                K.mm(ps[:, 0:wdt], wu[:, kt, gv * 256 + fi * 128:gv * 256 + (fi + 1) * 128], aT[:, kt, b0:b0 + wdt],
                                     start=(kt == 0), stop=(kt == KT - 1))
                            raw = raws[gv]; u = us[gv]
                            K.copy(raw[:, 0:wdt], ps[:, 0:wdt], eng="act")
                            if b0 == 0:
                                K.ts(raw[:, 0:2], raw[:, 0:2], cw[:, FT * 6:FT * 6 + 1], None, ALU.mult)
                            cb_ = ft * 6 + gv * 3
                            K.act(u[:, 0:wdt - 2], raw[:, 2:wdt], AF.Copy, scale=cw[:, cb_ + 2:cb_ + 3])
                            K.stt(u[:, 0:wdt - 2], raw[:, 1:wdt - 1], cw[:, cb_ + 1:cb_ + 2], u[:, 0:wdt - 2], ALU.mult, ALU.add)
                            K.stt(u[:, 0:wdt - 2], raw[:, 0:wdt - 2], cw[:, cb_:cb_ + 1], u[:, 0:wdt - 2], ALU.mult, ALU.add,
                                  eng="dve")
                        K.act(us[0][:, 0:wdt - 2], us[0][:, 0:wdt - 2], AF.Silu)
                        K.tt(actT[:, ft, b0:b0 + wdt - 2], us[0][:, 0:wdt - 2], us[1][:, 0:wdt - 2], ALU.mult)
            K.emit()
        K.stack = st
        with ExitStack() as sc:
            K.stack = sc
            wd = K.sb([128, FT, 512], BF16, name="wd")
            xts = [K.sb([128, 512], F32, name=f"x1t{i}") for i in range(2)]
            hts = [K.sb([128, 512], F32, name=f"h2t{i}") for i in range(2)]
            xi = 0
            for cb in range(4):
                if RST:
                    K.reset()
                for k0 in range(0, FT, 11):
                    K.dma("pool", View(wd, wd.h[:, k0:k0 + 11, :]),
                          View(WD, w_down[k0 * 128:(k0 + 11) * 128, cb * 512:(cb + 1) * 512].rearrange("(kt p) n -> p kt n", p=128)))
                for (r0, n) in otiles:
                    ps = K.bank()
                    for kt in range(FT):
                        K.mm(ps[0:n, 0:512], actT[:, kt, r0:r0 + n], wd[:, kt, :], start=(kt == 0), stop=(kt == FT - 1))
                    xt = xts[xi % 2]; ht = hts[xi % 2]; xi += 1
                    K.dma("sp", xt[0:n, :], X1S[2 + r0:2 + r0 + n, cb * 512:(cb + 1) * 512])
                    K.stt(ht[0:n, :], xt[0:n, :], ALPHA, ps[0:n, 0:512], ALU.mult, ALU.add)
                    K.dma("sp", H2S[r0:r0 + n, cb * 512:(cb + 1) * 512], ht[0:n, :])
            K.finish(hts + xts)
            K.emit()
        sact.close()
        K.stack = st
        with ExitStack() as sd:
            K.stack = sd
            lnw = K.sb([128, D], F32, name="lnw2"); lnb = K.sb([128, D], F32, name="lnb2")
            K.dma("sp", lnw[:], LNP[:, 2 * D:3 * D]); K.dma("sp", lnb[:], LNP[:, 3 * D:4 * D])
            x2 = K.sb([128, len(otiles), D], F32, name="x2")
            pT = K.sb([128, 2, NT], BF16, name="pT")
            pb = K.sb([128, 256], BF16, name="pb")
            for ti, (r0, n) in enumerate(otiles):
                if RST and ti % 4 == 0:
                    K.reset()
                hv = x2[0:n, ti, :]
                K.dma("sp", hv, H2S[r0:r0 + n, :])
                layer_norm_tile(K, hv, n, lnw[0:n, :], lnb[0:n, :], wk)
                K.copy(xbf[0:n, :], hv, eng="act")
                for g in range(KT // 4):
                    pt = K.tbank()
                    for j in range(4):
                        kt = g * 4 + j
                        K.tr(pt[:, j * 128:j * 128 + n], xbf[0:n, kt * 128:(kt + 1) * 128], K.identb[0:n, 0:n])
                    for j in range(4):
                        kt = g * 4 + j
                        K.copy(aT[:, kt, r0:r0 + n], pt[:, j * 128:j * 128 + n], eng=("act" if j % 2 else "dve"))
                K.dma("pool", pb[0:n, :], PL[r0:r0 + n, :])
                pt = K.tbank()
                for j in range(2):
                    K.tr(pt[:, j * 128:j * 128 + n], pb[0:n, j * 128:(j + 1) * 128], K.identb[0:n, 0:n])
                for j in range(2):
                    K.copy(pT[:, j, r0:r0 + n], pt[:, j * 128:j * 128 + n])
            wgs = [K.sb([128, KT, 512], BF16, name=f"wg{i}") for i in range(2)]
            wps = [K.sb([128, 2, 512], BF16, name=f"wp{i}") for i in range(2)]
            sgs = [K.sb([128, 512], F32, name=f"sg{i}") for i in range(2)]
            ots = [K.sb([128, 512], F32, name=f"ot{i}") for i in range(2)]
            xi = 0
            for cb in range(4):
                if RST:
                    K.reset()
                wg = wgs[cb % 2]; wp = wps[cb % 2]
                K.dma("pool", View(wg, wg.h[:]), View(WPG, w_pg[:, cb * 512:(cb + 1) * 512].rearrange("(kt p) n -> p kt n", p=128)))
                K.dma("pool", View(wp, wp.h[:]), View(WPP, w_pp[:, cb * 512:(cb + 1) * 512].rearrange("(kt p) n -> p kt n", p=128)))
                for ti, (r0, n) in enumerate(otiles):
                    ps = K.bank()
                    for kt in range(KT):
                        K.mm(ps[0:n, 0:512], aT[:, kt, r0:r0 + n], wg[:, kt, :], start=(kt == 0), stop=(kt == KT - 1))
                    ps2 = K.bank()
                    for kt in range(2):
                        K.mm(ps2[0:n, 0:512], pT[:, kt, r0:r0 + n], wp[:, kt, :], start=(kt == 0), stop=(kt == 1))
                    sg = sgs[xi % 2]; ot = ots[xi % 2]; xi += 1
                    K.act(sg[0:n, :], ps[0:n, 0:512], AF.Sigmoid)
                    K.tt(sg[0:n, :], sg[0:n, :], ps2[0:n, 0:512], ALU.mult)
                    K.tt(ot[0:n, :], sg[0:n, :], x2[0:n, ti, cb * 512:(cb + 1) * 512], ALU.add)
                    K.dma("sp", Y[r0:r0 + n, cb * 512:(cb + 1) * 512], ot[0:n, :])
            K.finish([Y] + ots)
            K.emit()
        K.stack = st
    return nc


def core_inputs0(h, x2d, p):
    w = p["e_w_in"][0]
    cols = []
    for j in range(4 * h, 4 * h + 4):
        for base in (0, 1024, 2048, 3072):
            cols.append(np.arange(base + j * 128, base + (j + 1) * 128))
    for j in range(2 * h, 2 * h + 2):
        cols += [np.arange(4096 + j * 128, 4096 + (j + 1) * 128), np.arange(4608 + j * 128, 4608 + (j + 1) * 128),
                 np.arange(5120 + j * 256, 5120 + (j + 1) * 256), np.arange(6160 + j * 256, 6160 + (j + 1) * 256)]
    cols.append(np.arange(6144, 6160))
    cols = np.concatenate(cols)
    wc = np.ascontiguousarray(w[:, cols])
    hs = slice(512 * h, 512 * h + 512); gs = slice(256 * h, 256 * h + 256)
    rows = [p["hgrn_lb_logits"][0, hs], p["hgrn_lb_logits"][1, hs], p["hgrn_lb_logits"][2, hs], p["e_hgrn_norm"][0, hs],
            p["e_gla_b"][0, gs], p["e_gla_norm"][0, hs]]
    prm = np.broadcast_to(np.concatenate(rows)[None, :], (64, sum(len(r) for r in rows)))
    w2 = np.zeros((64, 256), np.float32); w2[0:16] = p["e_gla_w2"][0][:, gs]
    prm = np.ascontiguousarray(np.concatenate([prm, w2], 1))
    return {"x": x2d, "w": wc, "prm": prm, "cst": CST}


def core_inputs1(h, x2d, p):
    w = p["o_w_in"][0]
    rs = slice(512 * h, 512 * h + 512)
    rcols = np.concatenate([np.arange(512 * h, 512 * h + 512), 1024 + np.arange(512 * h, 512 * h + 512),
                            2048 + np.arange(512 * h, 512 * h + 512), np.arange(3072, 3520)])
    wr = np.ascontiguousarray(w[:, rcols])
    gcols = []
    for j in range(4 * h, 4 * h + 4):
        gcols += [3520 + np.arange(j * 128, (j + 1) * 128), 3520 + 1024 + np.arange(j * 128, (j + 1) * 128),
                  3520 + 2048 + np.arange(j * 128, (j + 1) * 128), 3520 + 3072 + np.arange(j * 128, (j + 1) * 128),
                  np.array([3520 + 4096 + j]), np.array([3520 + 4104 + j])]
    wg = np.ascontiguousarray(w[:, np.concatenate(gcols)])
    rows = [p["o_rwkv_mu"][0][rcols], p["o_rwkv_w0"][0, rs], p["o_rwkv_a0"][0, rs], p["o_rwkv_kk"][0, rs], p["o_rwkv_ka"][0, rs],
            p["o_rwkv_rk"][0].reshape(-1)[rs], p["o_rwkv_gn_w"][0, rs], p["o_rwkv_gn_b"][0, rs]]
    for j in range(4 * h, 4 * h + 4):
        cw = p["o_gdn_conv"][0]
        for jj in range(4):
            rows += [cw[jj, j * 128:(j + 1) * 128], cw[jj, 1024 + j * 128:1024 + (j + 1) * 128], cw[jj, 2048 + j * 128:2048 + (j + 1) * 128]]
        rows += [p["o_gdn_a_log"][0, j:j + 1], p["o_gdn_dt_bias"][0, j:j + 1], p["o_gdn_norm"][0, j * 128:(j + 1) * 128]]
    row = np.concatenate(rows)
    prm = np.ascontiguousarray(np.broadcast_to(row[None, :], (128, row.shape[0])))
    lora = np.zeros((128, 2048), np.float32)
    lora[0:96, 0:512] = p["o_rwkv_w2"][0][:, rs]; lora[0:96, 512:1024] = p["o_rwkv_a2"][0][:, rs]
    lora[:, 1024:1536] = p["o_rwkv_g2"][0][0:128, rs]; lora[:, 1536:2048] = p["o_rwkv_g2"][0][128:256, rs]
    return {"x": x2d, "wr": wr, "wg": wg, "prm": prm, "lora": lora, "cst": CST}


def dense_inputs(L, w_out, xrows, orows, prow, flag, P):
    lnp = np.concatenate([P["ln_mix_w"][L], P["ln_mix_b"][L], P["ln_ffn_w"][L], P["ln_ffn_b"][L]])
    lnp = np.ascontiguousarray(np.broadcast_to(lnp[None, :], (128, lnp.shape[0])))
    cv = P["ffn_conv"][L]
    g = cv[:, :FF].reshape(3, FT, 128); v = cv[:, FF:].reshape(3, FT, 128)
    cwp = np.concatenate([g.transpose(2, 1, 0), v.transpose(2, 1, 0)], axis=2).reshape(128, FT * 6)
    cwp = np.ascontiguousarray(np.concatenate([cwp, np.full((128, 1), flag, np.float32)], 1))
    return {"o": np.ascontiguousarray(orows), "xr": np.ascontiguousarray(xrows), "pl": np.ascontiguousarray(prow),
            "w_out": w_out, "w_up": P["ffn_w_up"][L], "w_down": P["ffn_w_down"][L], "w_pg": P["ple_w_gate"][L],
            "w_pp": P["ple_w_proj"][L], "lnp": lnp, "cwp": cwp, "cst": CST}


_PROGS = {}
NCORES = 8
SEQ = 2048
HALF = SEQ // 2


def _prog(name):
    if name not in _PROGS:
        if name == "m0":
            _PROGS[name] = build_mixer0(SEQ)
        elif name == "m1":
            _PROGS[name] = build_mixer1(SEQ)
        else:
            _PROGS[name] = build_dense(HALF)
    return _PROGS[name]


def _assemble_o(res):
    B = NCORES // 2
    o = np.zeros((B, SEQ, 2048), dtype=ml_dtypes.bfloat16)
    for c in range(NCORES):
        b, h = c // 2, c % 2
        oc = res.results[c]["o"]
        o[b, :, 512 * h:512 * h + 512] = oc[:, 0:512]
        o[b, :, 1024 + 512 * h:1024 + 512 * h + 512] = oc[:, 512:1024]
    return o


def _dense_launch(L, w_out, x, o, p, P):
    in_maps = []
    for c in range(NCORES):
        b, h = c // 2, c % 2
        t0 = h * HALF
        if h == 0:
            xrows = np.concatenate([np.zeros((2, 2048), np.float32), x[b, 0:HALF]])
            orows = np.concatenate([np.zeros((2, 2048), ml_dtypes.bfloat16), o[b, 0:HALF]])
            flag = 0.0
        else:
            xrows = x[b, t0 - 2:t0 + HALF]
            orows = o[b, t0 - 2:t0 + HALF]
            flag = 1.0
        in_maps.append(dense_inputs(L, w_out, xrows, orows, p[L, b, t0:t0 + HALF], flag, P))
    res = run_bass_kernel_spmd(_prog("dense"), in_maps, core_ids=list(range(NCORES)))
    y = np.zeros((NCORES // 2, SEQ, 2048), np.float32)
    for c in range(NCORES):
        b, h = c // 2, c % 2
        y[b, h * HALF:(h + 1) * HALF] = res.results[c]["y"]
    return y


def kernel(**inputs):
    P = {k: np.asarray(v, dtype=np.float32) for k, v in inputs.items()}
    x = np.ascontiguousarray(P["x"]); p = P["p"]
    res = run_bass_kernel_spmd(_prog("m0"), [core_inputs0(c % 2, x[c // 2], P) for c in range(NCORES)],
                               core_ids=list(range(NCORES)))
    o0 = _assemble_o(res)
    x1 = _dense_launch(0, P["e_w_out"][0], x, o0, p, P)
    res = run_bass_kernel_spmd(_prog("m1"), [core_inputs1(c % 2, x1[c // 2], P) for c in range(NCORES)],
                               core_ids=list(range(NCORES)))
    o1 = _assemble_o(res)
    return _dense_launch(1, P["o_w_out"][0], x1, o1, p, P)
```
